# Optimizing a Trainium2 kernel written in Bass

```python
import jax, jax.numpy as jnp
from jax import lax
import numpy as np

D_MODEL = 1024
BATCH = 2
SEQ = 8192
DEPTH = 1

ATTN_WIDTH = D_MODEL // 2
ATTN_HEADS = 8
ATTN_HEAD_DIM = ATTN_WIDTH // ATTN_HEADS
DILATED_PAIRS = ((128, 1), (512, 4), (2048, 16))
ATTN_BLOCK = 128
ROPE_THETA = 10000.0
HGRN_WIDTH = D_MODEL - ATTN_WIDTH
HGRN_EXPAND = 128
HGRN_HEADS = HGRN_WIDTH // HGRN_EXPAND
HGRN_CHUNK = 16
MIX_WIDTH = ATTN_WIDTH + HGRN_WIDTH
IN_PROJ_WIDTH = 3 * ATTN_WIDTH + 4 * HGRN_WIDTH
FFN_HIDDEN = ((-(-8 * D_MODEL // 3) + 255) // 256) * 256
NORM_EPS = 1e-6

kernel_name = "hymba_dilated_attn_hgrn2_block"


def rmsnorm(x, w):
    xf = x.astype(jnp.float32)
    y = xf * lax.rsqrt(jnp.mean(xf * xf, axis=-1, keepdims=True) + NORM_EPS)
    return (y * w.astype(jnp.float32)).astype(x.dtype)


def rotary(x):
    S, Dh = x.shape[1], x.shape[3]
    half = Dh // 2
    inv_freq = ROPE_THETA ** (-jnp.arange(half, dtype=jnp.float32) / half)
    ang = jnp.arange(S, dtype=jnp.float32)[:, None] * inv_freq[None, :]
    cos = jnp.cos(ang)[None, :, None, :]
    sin = jnp.sin(ang)[None, :, None, :]
    xf = x.astype(jnp.float32)
    x1, x2 = xf[..., :half], xf[..., half:]
    return jnp.concatenate([x1 * cos - x2 * sin, x2 * cos + x1 * sin], axis=-1)


def dilated_window_attention(q, k, v, window, dilation):
    B, S, H, Dh = q.shape
    L = S // dilation
    W = window // dilation
    n_blk = -(-L // ATTN_BLOCK)
    Lp = n_blk * ATTN_BLOCK

    def to_blocks(t):
        t = t.reshape(B, L, dilation, H, Dh).transpose(0, 2, 1, 3, 4)
        t = jnp.pad(t, ((0, 0), (0, 0), (0, Lp - L), (0, 0), (0, 0)))
        return t.reshape(B, dilation, n_blk, ATTN_BLOCK, H, Dh)

    def with_prev(t):
        prev = jnp.pad(t, ((0, 0), (0, 0), (1, 0), (0, 0), (0, 0), (0, 0)))[:, :, :-1]
        return jnp.concatenate([prev, t], axis=3)

    qb = to_blocks(q)
    kc = with_prev(to_blocks(k))
    vc = with_prev(to_blocks(v))
    scores = jnp.einsum('bdnqhe,bdnkhe->bdnhqk', qb, kc) * (Dh ** -0.5)
    qi = jnp.arange(ATTN_BLOCK)[:, None]
    kj = jnp.arange(2 * ATTN_BLOCK)[None, :]
    delta = ATTN_BLOCK + qi - kj
    blk = jnp.arange(n_blk)[:, None, None]
    valid = (delta >= 0) & (delta <= W) & ((blk > 0) | (kj >= ATTN_BLOCK))[...]
    scores = jnp.where(valid[None, None, :, None], scores, -jnp.inf)
    m = jnp.max(scores, axis=-1, keepdims=True)
    p = jnp.exp(scores - m)
    s = jnp.sum(p, axis=-1, keepdims=True)
    out = jnp.einsum('bdnhqk,bdnkhe->bdnqhe', p, vc) / s.transpose(0, 1, 2, 4, 3, 5)
    lse = (m + jnp.log(s))[..., 0].transpose(0, 1, 2, 4, 3)
    out = out.reshape(B, dilation, Lp, H, Dh)[:, :, :L].transpose(0, 2, 1, 3, 4).reshape(B, S, H, Dh)
    lse = lse.reshape(B, dilation, Lp, H)[:, :, :L].transpose(0, 2, 1, 3).reshape(B, S, H)
    return out, lse


def dilated_attention_group(q, k, v):
    B, S, _ = q.shape
    qh = rotary(q.reshape(B, S, ATTN_HEADS, ATTN_HEAD_DIM))
    kh = rotary(k.reshape(B, S, ATTN_HEADS, ATTN_HEAD_DIM))
    vh = v.reshape(B, S, ATTN_HEADS, ATTN_HEAD_DIM).astype(jnp.float32)
    outs, lses = [], []
    for window, dilation in DILATED_PAIRS:
        o, l = dilated_window_attention(qh, kh, vh, window, dilation)
        outs.append(o)
        lses.append(l)
    weights = jax.nn.softmax(jnp.stack(lses, axis=0), axis=0)
    y = jnp.sum(weights[..., None] * jnp.stack(outs, axis=0), axis=0)
    return y.reshape(B, S, ATTN_WIDTH)


def hgrn2_group(q, f_logit, i, g, lb, norm_w):
    B, S, _ = q.shape
    H, Dk, C = HGRN_HEADS, HGRN_EXPAND, HGRN_CHUNK
    N = S // C
    f = lb + (1.0 - lb) * jax.nn.sigmoid(f_logit.astype(jnp.float32))
    log_f = jnp.log(f)
    key = 1.0 - f
    qf = jax.nn.silu(q.astype(jnp.float32))

    def chunks(t):
        return t.reshape(B, N, C, H, Dk).transpose(0, 3, 1, 2, 4)

    qc, kc, vc, lfc = chunks(qf), chunks(key), chunks(i.astype(jnp.float32)), chunks(log_f)
    b = jnp.cumsum(lfc, axis=3)
    causal = jnp.tril(jnp.ones((C, C), dtype=bool))
    diff = b[:, :, :, :, None, :] - b[:, :, :, None, :, :]
    decay = jnp.exp(jnp.where(causal[:, :, None], diff, -jnp.inf))
    scores = jnp.einsum('bhntd,bhnsd,bhntsd->bhnts', qc, kc, decay)
    o_intra = jnp.einsum('bhnts,bhnsv->bhntv', scores, vc)

    b_last = b[:, :, :, -1:, :]
    q_inter = qc * jnp.exp(b)
    k_upd = kc * jnp.exp(b_last - b)
    chunk_decay = jnp.exp(b_last[:, :, :, 0, :])

    def step(state, xs):
        qn, kn, vn, dn = xs
        o = jnp.einsum('bhtd,bhdv->bhtv', qn, state)
        state = dn[..., None] * state + jnp.einsum('bhtd,bhtv->bhdv', kn, vn)
        return state, o

    xs = (jnp.moveaxis(q_inter, 2, 0), jnp.moveaxis(k_upd, 2, 0),
          jnp.moveaxis(vc, 2, 0), jnp.moveaxis(chunk_decay, 2, 0))
    state0 = jnp.zeros((B, H, Dk, Dk), dtype=jnp.float32)
    _, o_inter = lax.scan(step, state0, xs)
    o = o_intra + jnp.moveaxis(o_inter, 0, 2)
    o = o.transpose(0, 2, 3, 1, 4).reshape(B, S, H, Dk)
    o = o * lax.rsqrt(jnp.mean(o * o, axis=-1, keepdims=True) + NORM_EPS)
    o = o.reshape(B, S, HGRN_WIDTH) * norm_w.astype(jnp.float32)
    return o * jax.nn.silu(g.astype(jnp.float32))


def setup_inputs(seed: int = 0) -> dict:
    key = jax.random.key(seed)
    ks = jax.random.split(key, 10)
    f32 = jnp.float32
    x = jax.random.normal(ks[0], (BATCH, SEQ, D_MODEL), f32)
    norm1_w = 1.0 + 0.02 * jax.random.normal(ks[1], (DEPTH, D_MODEL), f32)
    w_in = jax.random.normal(ks[2], (DEPTH, D_MODEL, IN_PROJ_WIDTH), f32) * D_MODEL ** -0.5
    lb_logits = 0.5 * jax.random.normal(ks[3], (DEPTH + 1, HGRN_WIDTH), f32)
    hgrn_norm_w = 1.0 + 0.02 * jax.random.normal(ks[4], (DEPTH, HGRN_WIDTH), f32)
    w_out = jax.random.normal(ks[5], (DEPTH, MIX_WIDTH, D_MODEL), f32) * MIX_WIDTH ** -0.5
    norm2_w = 1.0 + 0.02 * jax.random.normal(ks[6], (DEPTH, D_MODEL), f32)
    w_gate_up = jax.random.normal(ks[7], (DEPTH, D_MODEL, 2 * FFN_HIDDEN), f32) * D_MODEL ** -0.5
    w_down = jax.random.normal(ks[8], (DEPTH, FFN_HIDDEN, D_MODEL), f32) * FFN_HIDDEN ** -0.5
    final_norm_w = 1.0 + 0.02 * jax.random.normal(ks[9], (D_MODEL,), f32)
    return {"x": x, "norm1_w": norm1_w, "w_in": w_in, "lb_logits": lb_logits,
            "hgrn_norm_w": hgrn_norm_w, "w_out": w_out, "norm2_w": norm2_w,
            "w_gate_up": w_gate_up, "w_down": w_down, "final_norm_w": final_norm_w}


def reference(x, norm1_w, w_in, lb_logits, hgrn_norm_w, w_out, norm2_w, w_gate_up, w_down, final_norm_w):
    lb_table = jnp.cumsum(jax.nn.softmax(lb_logits.astype(jnp.float32), axis=0), axis=0)
    h = x
    for l in range(DEPTH):
        u = rmsnorm(h, norm1_w[l])
        proj = jnp.einsum('bsd,de->bse', u, w_in[l])
        a = ATTN_WIDTH
        qa, ka, va = proj[..., :a], proj[..., a:2 * a], proj[..., 2 * a:3 * a]
        o = 3 * a
        w = HGRN_WIDTH
        qb, fb, ib, gb = (proj[..., o:o + w], proj[..., o + w:o + 2 * w],
                          proj[..., o + 2 * w:o + 3 * w], proj[..., o + 3 * w:o + 4 * w])
        ya = dilated_attention_group(qa, ka, va)
        yb = hgrn2_group(qb, fb, ib, gb, lb_table[l], hgrn_norm_w[l])
        mixed = jnp.concatenate([ya, yb], axis=-1).astype(h.dtype)
        h = h + jnp.einsum('bse,ed->bsd', mixed, w_out[l])
        u2 = rmsnorm(h, norm2_w[l])
        gu = jnp.einsum('bsd,df->bsf', u2, w_gate_up[l])
        gate, up = gu[..., :FFN_HIDDEN], gu[..., FFN_HIDDEN:]
        h = h + jnp.einsum('bsf,fd->bsd', jax.nn.silu(gate) * up, w_down[l])
    return rmsnorm(h, final_norm_w)
```

```python
import contextlib
import numpy as np
import ml_dtypes
import concourse.bass as bass
import concourse.mybir as mybir
from concourse.bass_utils import run_bass_kernel_spmd

F32 = mybir.dt.float32
BF16 = mybir.dt.bfloat16
AF = mybir.ActivationFunctionType
ALU = mybir.AluOpType

ENGS = ("sync", "scalar", "vector", "gpsimd", "tensor")

D = 1024
TOWN = 2048
TALL = 4096
FF = 2816
NFC = FF // 128
EPS = 1e-6
NCORES = 8

PATTERNS = ((1, 16), (4, 4), (16, 1))
VT_INDEX = {}
for _d, _nb in PATTERNS:
    for _r in range(_d):
        for _ip in range(_nb + 1):
            VT_INDEX[(_d, _r, _ip)] = len(VT_INDEX)
NVT = len(VT_INDEX)


def _sl(start, n, step):
    return slice(start, start + (n - 1) * step + 1, step)


class Res:
    __slots__ = ("name", "writer", "readers", "wsem", "wcnt", "rsem", "rcnt")

    def __init__(self, name):
        self.name = name
        self.writer = None
        self.readers = {}
        self.wsem = None
        self.wcnt = 0
        self.rsem = None
        self.rcnt = 0


class Sched:
    def __init__(self, nc):
        self.nc = nc
        self.prog = {e: [] for e in ENGS}
        self.sem = {}
        self.cnt = {e: 0 for e in ENGS}
        self.pending = {e: False for e in ENGS}
        self.seen = {e: {} for e in ENGS}
        self.latest = {}
        self.nsem = 0
        self.ninst = {e: 0 for e in ENGS}
        self.nwait = {e: 0 for e in ENGS}
        self.skip = False
        self.mark_n = None
        self.limit = None
        self.last_op_desc = None
        for e in ENGS:
            self.sem[e] = nc.alloc_semaphore(name=f"es_{e}")

    def mark(self, limit):
        self.mark_n = 0
        self.limit = limit

    def _tick(self, desc):
        if self.skip:
            return True
        if self.mark_n is not None and self.limit is not None:
            if self.mark_n >= self.limit:
                self.skip = True
                print("SKIP starts before op:", desc, "last:", self.last_op_desc)
                if self.pending["tensor"]:
                    self.cnt["tensor"] += 1
                    self.pending["tensor"] = False
                    sem = self.sem["tensor"]
                    self.prog["tensor"].append(lambda e, sem=sem: e.nop().then_inc(sem, 1))
                return True
            self.mark_n += 1
            self.last_op_desc = desc
        return False

    def newsem(self, name):
        self.nsem += 1
        k = f"d{self.nsem}_{name}"
        self.sem[k] = self.nc.alloc_semaphore(name=k)
        return k

    def _deps(self, reads, writes):
        deps = {}

        def add(k, v):
            if deps.get(k, 0) < v:
                deps[k] = v
        for r in reads:
            if r.writer is not None:
                add(*r.writer)
        for w in writes:
            if w.writer is not None:
                add(*w.writer)
            for k, v in w.readers.items():
                add(k, v)
        return deps

    def _emit_waits(self, eng, deps):
        for k, v in deps.items():
            if k == eng and eng == "tensor":
                continue
            if k in ENGS and v > self.cnt[k]:
                raise RuntimeError(f"dependency on unsignaled {k} instr (from {eng})")
            if self.seen[eng].get(k, 0) >= v:
                continue
            self.seen[eng][k] = v
            sem = self.sem[k]
            self.prog[eng].append(lambda e, sem=sem, v=v: e.wait_ge(sem, v))
            self.nwait[eng] += 1

    def _mark(self, tok, reads, writes):
        k, v = tok
        for r in reads:
            if r.readers.get(k, 0) < v:
                r.readers[k] = v
        for w in writes:
            w.writer = tok
            w.readers = {}
        if self.latest.get(k, 0) < v:
            self.latest[k] = v

    def op(self, eng, fn, reads=(), writes=(), signal=True, desc=None):
        if self._tick((eng, desc, [w.name for w in writes])):
            return None
        deps = self._deps(reads, writes)
        self._emit_waits(eng, deps)
        self.ninst[eng] += 1
        if signal:
            self.cnt[eng] += 1
            self.pending[eng] = False
            sem = self.sem[eng]
            self.prog[eng].append(lambda e, fn=fn, sem=sem: fn(e).then_inc(sem, 1))
            tok = (eng, self.cnt[eng])
        else:
            self.pending[eng] = True
            self.prog[eng].append(lambda e, fn=fn: fn(e))
            tok = (eng, self.cnt[eng] + 1)
        self._mark(tok, reads, writes)
        return tok

    def dma(self, parts, reads=(), writes=(), q="sync"):
        if self._tick(("dma", [w.name for w in writes], [r.name for r in reads])):
            return None
        deps = self._deps(reads, writes)
        self._emit_waits(q, deps)
        if writes:
            r = writes[0]
            if r.wsem is None:
                r.wsem = self.newsem("w_" + r.name)
            semk = r.wsem
            r.wcnt += 16 * len(parts)
            val = r.wcnt
        else:
            r = reads[0]
            if r.rsem is None:
                r.rsem = self.newsem("r_" + r.name)
            semk = r.rsem
            r.rcnt += 16 * len(parts)
            val = r.rcnt
        sem = self.sem[semk]
        for (o, i) in parts:
            self.prog[q].append(
                lambda e, o=o, i=i, sem=sem: e.dma_start(out=o, in_=i).then_inc(sem, 16))
        self.ninst[q] += len(parts)
        tok = (semk, val)
        self._mark(tok, reads, writes)
        return tok

    def async_op(self, eng, fn, reads=(), writes=()):
        if self._tick((eng, "async", [w.name for w in writes])):
            return None
        deps = self._deps(reads, writes)
        self._emit_waits(eng, deps)
        r = writes[0]
        if r.wsem is None:
            r.wsem = self.newsem("a_" + r.name)
        semk = r.wsem
        r.wcnt += 1
        val = r.wcnt
        sem = self.sem[semk]
        self.prog[eng].append(lambda e, fn=fn, sem=sem: fn(e).then_inc(sem, 1))
        self.ninst[eng] += 1
        tok = (semk, val)
        self._mark(tok, reads, writes)
        return tok

    def wait_tok(self, eng, tok):
        if tok is not None:
            self._emit_waits(eng, {tok[0]: tok[1]})

    def barrier(self, engines=ENGS):
        for e in engines:
            assert not self.pending[e]
        snap = dict(self.latest)
        for e in engines:
            self._emit_waits(e, snap)

    def run(self):
        nc = self.nc
        for e in ENGS:
            assert not self.pending[e], f"unsignaled tail on {e}"
        with nc.Block() as block:
            for e in ENGS:
                lst = self.prog[e]
                if not lst:
                    continue

                def body(engine, lst=lst):
                    for th in lst:
                        th(engine)
                getattr(block, e)(body)


class PsumPool:
    def __init__(self, nc):
        self.banks = []
        for i in range(8):
            ap = nc.alloc_psum_tensor(f"psb{i}", [128, 512], F32).ap()
            self.banks.append((ap, Res(f"psb{i}")))
        self.i = 0

    def get(self):
        b = self.banks[self.i % 8]
        self.i += 1
        return b


def build_nc(dbg=None, stop_after=99):
    nc = bass.Bass("TRN2", target_bir_lowering=False)
    S = Sched(nc)
    PS = PsumPool(nc)

    def din(name, shape, dt=F32):
        return nc.dram_tensor(name, list(shape), dt, kind="ExternalInput").ap()

    def dscr(name, shape, dt):
        return nc.dram_tensor(name, list(shape), dt, kind="Internal").ap()

    x_all = din("x_all", [TALL, D])
    w_inT = din("w_inT", [37, 128, 8, 128])
    w_outL = din("w_outL", [128, 9, 1024])
    w_guT = din("w_guT", [NFC + 1, 2, 128, 8, 128])
    w_downL = din("w_downL", [128, NFC + 1, 1024])
    cosT = din("cosT", [128, TALL])
    sinT = din("sinT", [128, TALL])
    c_ident = din("c_ident", [128, 128])
    c_amask = din("c_amask", [128, 512])
    c_hmask = din("c_hmask", [128, 512])
    n1 = din("n1", [128, 8])
    n2 = din("n2", [128, 8])
    hnw = din("hnw", [128, 4])
    lbl = din("lbl", [128, 8])
    fw = din("fw", [D])
    pm = din("pm", [128, 8])
    kval = din("kval", [128, NVT])
    out = nc.dram_tensor("out", [TOWN, D], F32, kind="ExternalOutput").ap()

    w_in_bf = dscr("w_in_bf", [36, 128, 8, 128], BF16)
    w_out_bf = dscr("w_out_bf", [128, 8, 1024], BF16)
    w_gu_bf = dscr("w_gu_bf", [NFC, 2, 128, 8, 128], BF16)
    w_down_bf = dscr("w_down_bf", [128, NFC, 1024], BF16)
    cc_in = [dscr(f"cc_in{h}", [128, 129], F32) for h in range(4)]
    cc_out = [dscr(f"cc_out{h}", [8 * 128, 129], F32) for h in range(4)]

    dbg_out = {}

    def dbg_tensor(name, shape, dt):
        t = nc.dram_tensor(name, list(shape), dt, kind="ExternalOutput").ap()
        dbg_out[name] = t
        return t

    def act(out_, in_, func, reads, writes, **kw):
        return S.op("scalar", lambda e: e.activation(out=out_, in_=in_, func=func, **kw), reads, writes)

    def tt(eng, out_, in0, in1, op, reads, writes):
        return S.op(eng, lambda e: e.tensor_tensor(out=out_, in0=in0, in1=in1, op=op), reads, writes)

    def ts(eng, out_, in0, s1, s2, op0, op1, reads, writes):
        if s2 is None:
            return S.op(eng, lambda e: e.tensor_scalar(out=out_, in0=in0, scalar1=s1, scalar2=None, op0=op0),
                        reads, writes)
        return S.op(eng, lambda e: e.tensor_scalar(out=out_, in0=in0, scalar1=s1, scalar2=s2, op0=op0, op1=op1),
                    reads, writes)

    def stt(out_, in0, scalar, in1, op0, op1, reads, writes):
        return S.op("vector", lambda e: e.scalar_tensor_tensor(out=out_, in0=in0, scalar=scalar, in1=in1,
                                                               op0=op0, op1=op1), reads, writes)

    def cp(eng, out_, in_, reads, writes):
        return S.op(eng, lambda e: e.tensor_copy(out=out_, in_=in_), reads, writes)

    def mm(out_, lhsT, rhs, start, stop, reads, writes, signal):
        return S.op("tensor", lambda e: e.matmul(out_, lhsT=lhsT, rhs=rhs, start=start, stop=stop),
                    reads, writes, signal=signal)

    def mm_group(out_, pairs, reads, writes):
        n = len(pairs)
        for i, (l, r) in enumerate(pairs):
            mm(out_, l, r, i == 0, i == n - 1, reads, writes, signal=(i == n - 1))

    def tr(out_, in_, ident_, reads, writes, signal=True):
        return S.op("tensor", lambda e: e.transpose(out_, in_, ident_), reads, writes, signal=signal)

    def memset(eng, ap, val, writes):
        return S.op(eng, lambda e: e.memset(ap, val), (), writes)

    out_stores = []

    with contextlib.ExitStack() as glob:
        def gsb(name, shape, dt):
            return glob.enter_context(nc.sbuf_tensor(name, list(shape), dt)).ap()

        ident = gsb("ident", [128, 128], BF16)
        amask = gsb("amask", [128, 512], BF16)
        hmask = gsb("hmask", [128, 512], BF16)
        ones_bf = gsb("ones_bf", [128, 128], BF16)
        ones_f = gsb("ones_f", [128, 64], F32)
        n1s = gsb("n1s", [128, 8], F32)
        n2s = gsb("n2s", [128, 8], F32)
        hnws = gsb("hnws", [128, 4], F32)
        lbls = gsb("lbls", [128, 8], F32)
        lbc = gsb("lbc", [128, 4], F32)
        omlb = gsb("omlb", [128, 4], F32)
        pms = gsb("pms", [128, 8], F32)
        ompm = gsb("ompm", [128, 8], F32)
        mixT = gsb("mixT", [128, 8, TOWN], BF16)
        r_const = Res("const")
        r_mix = [Res(f"mix{c}") for c in range(8)]

        r_win = Res("w_in_bf")
        r_wout = Res("w_out_bf")
        r_wgu = Res("w_gu_bf")
        r_wdown = Res("w_down_bf")
        S.wait_tok("gpsimd", S.dma([(w_in_bf[i * 6:(i + 1) * 6], w_inT[i * 6:(i + 1) * 6]) for i in range(6)],
                                   writes=[r_win], q="gpsimd"))
        S.wait_tok("gpsimd", S.dma([(ident, c_ident), (amask, c_amask), (hmask, c_hmask)],
                                   writes=[r_const], q="gpsimd"))
        S.dma([(n1s, n1), (n2s, n2), (hnws, hnw), (lbls, lbl), (pms, pm)], writes=[r_const])
        memset("vector", ones_bf, 1.0, [r_const])
        memset("vector", ones_f, 1.0, [r_const])
        tt("vector", lbc, lbls[:, 0:4], lbls[:, 4:8], ALU.subtract, [r_const], [r_const])
        act(lbc, lbc, AF.Sigmoid, [r_const], [r_const])
        ts("vector", omlb, lbc, -1.0, 1.0, ALU.mult, ALU.add, [r_const], [r_const])
        ts("vector", ompm, pms, -1.0, 1.0, ALU.mult, ALU.add, [r_const], [r_const])

        with contextlib.ExitStack() as ph14:
            uT = ph14.enter_context(nc.sbuf_tensor("uT", [128, 8, TALL], BF16)).ap()
            r_uT = [Res(f"uT{b}") for b in range(8)]

            with contextlib.ExitStack() as ph1:
                def sb1(name, shape, dt):
                    return ph1.enter_context(nc.sbuf_tensor(name, list(shape), dt)).ap()
                xb = [sb1(f"xb{i}", [128, D], F32) for i in range(3)]
                r_xb = [Res(f"xb{i}") for i in range(3)]
                xn = [sb1(f"xn{i}", [128, D], BF16) for i in range(2)]
                r_xn = [Res(f"xn{i}") for i in range(2)]
                junk = sb1("junk", [128, D], BF16)
                r_junk = Res("junk")
                ss = sb1("ss", [128, 32], F32)
                rs = sb1("rs", [128, 32], F32)
                r_ss = [Res(f"ss{i}") for i in range(32)]
                for i in range(32):
                    k = i % 3
                    S.dma([(xb[k], x_all[128 * i:128 * (i + 1), :])], writes=[r_xb[k]])
                    act(junk, xb[k], AF.Square, [r_xb[k]], [r_junk, r_ss[i]], accum_out=ss[:, i:i + 1])
                    act(rs[:, i:i + 1], ss[:, i:i + 1], AF.Ln, [r_ss[i]], [r_ss[i]], scale=1.0 / D, bias=EPS)
                    act(rs[:, i:i + 1], rs[:, i:i + 1], AF.Exp, [r_ss[i]], [r_ss[i]], scale=-0.5)
                    kk = i % 2
                    act(xn[kk], xb[k], AF.Copy, [r_xb[k], r_ss[i]], [r_xn[kk]], scale=rs[:, i:i + 1])
                    pb, r_pb = PS.get()
                    pbb = pb.bitcast(BF16)
                    for c in range(8):
                        tr(pbb[:, 128 * c:128 * (c + 1)], xn[kk][:, 128 * c:128 * (c + 1)], ident,
                           [r_xn[kk], r_const], [r_pb], signal=(c == 7))
                    tt("vector", uT[:, :, 128 * i:128 * (i + 1)],
                       pbb.rearrange("p (c t) -> p c t", c=8),
                       n1s[:, :].unsqueeze(2).broadcast_to([128, 8, 128]), ALU.mult,
                       [r_pb, r_const], [r_uT[i // 4]])
                S.barrier()
            if dbg == "uT":
                t = dbg_tensor("dbg_uT", [128, 8, TALL], BF16)
                out_stores.append(S.dma([(t, uT)], reads=r_uT))
            if stop_after <= 1:
                return _finish(nc, S, out_stores, dbg_out)

            with contextlib.ExitStack() as ph2:
                def sb2(name, shape, dt):
                    return ph2.enter_context(nc.sbuf_tensor(name, list(shape), dt)).ap()
                wset = [sb2(f"wset{i}", [128, 5, 8, 128], BF16) for i in range(2)]
                r_wset = [Res(f"wset{i}") for i in range(2)]
                QT = sb2("QT", [128, TOWN], BF16)
                KT = sb2("KT", [128, TALL], BF16)
                VT = sb2("VT", [128, TALL], BF16)
                r_QT, r_KT, r_VT = Res("QT"), Res("KT"), Res("VT")
                vaug = sb2("vaug", [128, NVT, 65], BF16)
                r_vaug = Res("vaug")
                acc = sb2("acc", [65, TOWN], F32)
                r_acc = Res("acc")
                PT = sb2("PT", [128, 18, 256], BF16)
                r_PT = [Res(f"PT{i}") for i in range(9)]
                tmp1 = sb2("tmp1", [128, 512], F32)
                tmp2 = sb2("tmp2", [128, 512], F32)
                r_tmp1, r_tmp2 = Res("tmp1"), Res("tmp2")
                ctb = [sb2(f"ctb{i}", [128, 512], F32) for i in range(2)]
                stb = [sb2(f"stb{i}", [128, 512], F32) for i in range(2)]
                r_tb = [Res(f"tb{i}") for i in range(2)]
                lnd = sb2("lnd", [65, TOWN], F32)
                r_lnd = Res("lnd")
                rcp = sb2("rcp", [64, 512], F32)
                r_rcp = Res("rcp")
                hbt = sb2("hbt", [64, TOWN], BF16)
                r_hbt = Res("hbt")
                kvs = sb2("kvs", [128, NVT], F32)
                r_kvs = Res("kvs")
                S.dma([(kvs, kval)], writes=[r_kvs])
                cp("vector", vaug[:, :, 64:65], kvs[:, :].unsqueeze(2), [r_kvs], [r_vaug])

                def load_wset(hp):
                    k = hp % 2
                    tiles = [hp, 28 + hp, 4 + hp, 32 + hp, 8 + hp]
                    S.dma([(wset[k][:, i, :, :], w_in_bf[tl]) for i, tl in enumerate(tiles)],
                          reads=[r_win], writes=[r_wset[k]])

                load_wset(0)
                tbi = 0
                S.wait_tok("gpsimd", S.dma([(w_out_bf[i * 32:(i + 1) * 32], w_outL[i * 32:(i + 1) * 32, 0:8, :])
                                            for i in range(4)], writes=[r_wout], q="gpsimd"))
                S.wait_tok("gpsimd", S.dma([(w_gu_bf[i * 2:(i + 1) * 2], w_guT[i * 2:(i + 1) * 2])
                                            for i in range(11)], writes=[r_wgu], q="gpsimd"))
                S.wait_tok("gpsimd", S.dma([(w_down_bf[i * 16:(i + 1) * 16], w_downL[i * 16:(i + 1) * 16, 0:NFC, :])
                                            for i in range(8)], writes=[r_wdown], q="gpsimd"))
                for hp in range(4):
                    if hp + 1 < 4:
                        load_wset(hp + 1)
                    ws = wset[hp % 2]
                    r_ws = r_wset[hp % 2]
                    for tb in range(8):
                        blk = slice(512 * tb, 512 * (tb + 1))
                        k = tbi % 2
                        tbi += 1
                        S.dma([(ctb[k], cosT[:, blk]), (stb[k], sinT[:, blk])], writes=[r_tb[k]])
                        targets = [(2, 3, KT, r_KT, blk)]
                        if tb >= 4:
                            targets.append((0, 1, QT, r_QT, slice(512 * (tb - 4), 512 * (tb - 3))))
                        for (wi, wsi, dst, r_dst, dblk) in targets:
                            pa, r_pa = PS.get()
                            mm_group(pa, [(ws[:, wi, c, :], uT[:, c, blk]) for c in range(8)],
                                     [r_ws, r_uT[tb]], [r_pa])
                            pb, r_pb = PS.get()
                            mm_group(pb, [(ws[:, wsi, c, :], uT[:, c, blk]) for c in range(8)],
                                     [r_ws, r_uT[tb]], [r_pb])
                            tt("vector", tmp1, pa, ctb[k], ALU.mult, [r_pa, r_tb[k]], [r_tmp1])
                            tt("vector", tmp2, pb, stb[k], ALU.mult, [r_pb, r_tb[k]], [r_tmp2])
                            tt("vector", dst[:, dblk], tmp1, tmp2, ALU.add, [r_tmp1, r_tmp2], [r_dst])
                        pv, r_pv = PS.get()
                        mm_group(pv, [(ws[:, 4, c, :], uT[:, c, blk]) for c in range(8)],
                                 [r_ws, r_uT[tb]], [r_pv])
                        act(VT[:, blk], pv, AF.Copy, [r_pv], [r_VT])
                    if dbg == "qkv" and hp == 0:
                        t = dbg_tensor("dbg_QT", [128, TOWN], BF16)
                        out_stores.append(S.dma([(t, QT)], reads=[r_QT]))
                        t = dbg_tensor("dbg_KT", [128, TALL], BF16)
                        out_stores.append(S.dma([(t, KT)], reads=[r_KT]))
                        t = dbg_tensor("dbg_VT", [128, TALL], BF16)
                        out_stores.append(S.dma([(t, VT)], reads=[r_VT]))
                    for hl in range(2):
                        hb = 64 * hl
                        specs = sorted(VT_INDEX.items(), key=lambda kv: kv[1])
                        for g in range(0, NVT, 8):
                            grp = specs[g:g + 8]
                            pb, r_pb = PS.get()
                            pbb = pb.bitcast(BF16)
                            for j, ((d, r, ip), vt) in enumerate(grp):
                                span = 128 * d
                                t0 = TOWN - span + span * ip + r
                                tr(pbb[:, 64 * j:64 * (j + 1)],
                                   VT[hb:hb + 64, _sl(t0, 128, d)], ident[hb:hb + 64, hb:hb + 64],
                                   [r_VT, r_const], [r_pb], signal=(j == len(grp) - 1))
                            ng = len(grp)
                            act(vaug[:, g:g + ng, 0:64],
                                pbb[:, 0:64 * ng].rearrange("p (j e) -> p j e", e=64),
                                AF.Copy, [r_pb], [r_vaug])
                        first = True
                        for (d, nb) in PATTERNS:
                            span = 128 * d
                            for r in range(d):
                                nkb = nb + 1
                                for g in range(0, nkb, 2):
                                    ps, r_ps = PS.get()
                                    for j in range(2):
                                        ip = g + j
                                        if ip >= nkb:
                                            continue
                                        tk = TOWN - span + span * ip + r
                                        keys = KT[hb:hb + 64, _sl(tk, 128, d)]
                                        if ip == 0:
                                            q0, nq, c0 = r, 128, 128
                                        elif ip == nb:
                                            q0, nq, c0 = span * (ip - 1) + r, 128, 0
                                        else:
                                            q0, nq, c0 = span * (ip - 1) + r, 256, 0
                                        mm(ps[:, 256 * j + c0:256 * j + c0 + nq], keys,
                                           QT[hb:hb + 64, _sl(q0, nq, d)], True, True,
                                           [r_KT, r_QT], [r_ps], signal=True)
                                    pt = PT[:, g:g + 2, :].rearrange("p a b -> p (a b)")
                                    r_pt = r_PT[g // 2]
                                    lo = 128 if g == 0 else 0
                                    last = min(g + 1, nkb - 1)
                                    hi = 256 * (last - g) + (128 if last == nb else 256)
                                    act(pt[:, lo:hi], ps[:, lo:hi], AF.Exp, [r_ps], [r_pt], scale=0.125)
                                    tt("vector", pt[:, lo:hi], pt[:, lo:hi], amask[:, lo:hi], ALU.mult,
                                       [r_pt, r_const], [r_pt])
                                for g in range(0, nb, 4):
                                    po, r_po = PS.get()
                                    nq4 = min(4, nb - g)
                                    for j in range(nq4):
                                        q = g + j
                                        oo = po[0:65, 128 * j:128 * (j + 1)]
                                        mm(oo, vaug[:, VT_INDEX[(d, r, q + 1)], :], PT[:, q + 1, 0:128],
                                           True, False, [r_vaug, r_PT[(q + 1) // 2]], [r_po], signal=False)
                                        mm(oo, vaug[:, VT_INDEX[(d, r, q)], :], PT[:, q, 128:256],
                                           False, True, [r_vaug, r_PT[q // 2]], [r_po], signal=(j == nq4 - 1))
                                    if d == 1:
                                        dst = acc[:, 512 * (g // 4):512 * (g // 4 + 1)]
                                        src = po[0:65, :]
                                    elif d == 4:
                                        assert nq4 == 4
                                        dst = acc[:, :].rearrange("p (q r) -> p r q", r=4)[:, r, :]
                                        src = po[0:65, :]
                                    else:
                                        dst = acc[:, :].rearrange("p (q r) -> p r q", r=16)[:, r, :]
                                        src = po[0:65, 0:128]
                                    if first:
                                        cp("vector", dst, src, [r_po], [r_acc])
                                    else:
                                        tt("vector", dst, src, dst, ALU.add, [r_po, r_acc], [r_acc])
                            first = False
                        if dbg == "acc" and hp == 0 and hl == 0:
                            t = dbg_tensor("dbg_acc", [65, TOWN], F32)
                            out_stores.append(S.dma([(t, acc)], reads=[r_acc]))
                        act(lnd[64:65, :], acc[64:65, :], AF.Ln, [r_acc], [r_lnd])
                        for b4 in range(4):
                            bl = slice(512 * b4, 512 * (b4 + 1))
                            pl, r_pl = PS.get()
                            mm(pl[0:64, :], ones_f[64:65, :], lnd[64:65, bl], True, True,
                               [r_const, r_lnd], [r_pl], signal=True)
                            act(rcp, pl[0:64, :], AF.Exp, [r_pl], [r_rcp], scale=-1.0)
                            if hl == 0:
                                tt("vector", mixT[0:64, hp, bl], acc[0:64, bl], rcp, ALU.mult,
                                   [r_acc, r_rcp], [r_mix[hp]])
                            else:
                                tt("vector", hbt[:, bl], acc[0:64, bl], rcp, ALU.mult,
                                   [r_acc, r_rcp], [r_hbt])
                        if hl == 1:
                            S.dma([(mixT[64:128, hp, :], hbt)], reads=[r_hbt], writes=[r_mix[hp]])
                S.barrier()
            if dbg == "attn":
                t = dbg_tensor("dbg_mixA", [128, 4, TOWN], BF16)
                out_stores.append(S.dma([(t, mixT[:, 0:4, :])], reads=r_mix[0:4]))
            if stop_after <= 2:
                return _finish(nc, S, out_stores, dbg_out)

            with contextlib.ExitStack() as ph3:
                def sb3(name, shape, dt):
                    return ph3.enter_context(nc.sbuf_tensor(name, list(shape), dt)).ap()
                hws = [sb3(f"hws{i}", [128, 4, 8, 128], BF16) for i in range(2)]
                r_hws = [Res(f"hws{i}") for i in range(2)]
                o_p = [sb3(f"o_p{i}", [128, TOWN], F32) for i in range(2)]
                qb_p = [sb3(f"qb_p{i}", [128, TOWN], BF16) for i in range(2)]
                sg_p = [sb3(f"sg_p{i}", [128, TOWN], BF16) for i in range(2)]
                r_op = [Res(f"o_p{i}") for i in range(2)]
                r_qbp = [Res(f"qb_p{i}") for i in range(2)]
                r_sgp = [Res(f"sg_p{i}") for i in range(2)]
                G = [sb3(f"G{i}", [128, 8, 129], F32) for i in range(2)]
                r_G = [Res(f"G{i}") for i in range(2)]
                ccs = [sb3(f"ccs{i}", [128, 129], F32) for i in range(2)]
                r_ccs = [Res(f"ccs{i}") for i in range(2)]
                r_ccin = [Res(f"ccin{h}") for h in range(4)]
                r_ccout = [Res(f"ccout{h}") for h in range(4)]
                fS = sb3("fS", [128, 512], F32)
                lgf = sb3("lgf", [128, 512], F32)
                Pc = sb3("Pc", [128, 512], F32)
                qf = sb3("qf", [128, 512], F32)
                key = sb3("key", [128, 512], F32)
                bb = sb3("bb", [128, 512], F32)
                ee = sb3("ee", [128, 512], F32)
                r_fS, r_lgf, r_Pc, r_qf, r_key, r_bb, r_ee = (Res(n) for n in
                                                               ("fS", "lgf", "Pc", "qf", "key", "bb", "ee"))
                qt_ = sb3("qt_", [128, 512], BF16)
                kt_ = sb3("kt_", [128, 512], BF16)
                khT = sb3("khT", [128, 512], BF16)
                khat = sb3("khat", [128, 4, 128], BF16)
                ATm = sb3("ATm", [128, 512], BF16)
                Vh = sb3("Vh", [128, 4, 128], BF16)
                r_qt, r_kt, r_khT, r_khat, r_ATm, r_Vh = (Res(n) for n in
                                                          ("qt_", "kt_", "khT", "khat", "ATm", "Vh"))
                Rc = sb3("Rc", [128, 8], F32)
                Pe = sb3("Pe", [128, 8], F32)
                dec = sb3("dec", [128, 8], F32)
                carry = sb3("carry", [128, 1], F32)
                r_small = Res("hsmall")
                S32 = sb3("S32", [128, 128], F32)
                r_S32 = Res("S32")
                Sbf = [sb3(f"Sbf{i}", [128, 128], BF16) for i in range(2)]
                r_Sbf = [Res(f"Sbf{i}") for i in range(2)]
                Sin = sb3("Sin", [128, 128], F32)
                Sinb = sb3("Sinb", [128, 128], BF16)
                tmpS = sb3("tmpS", [128, 128], F32)
                deff = sb3("deff", [128, 8], F32)
                r_Sin = Res("Sin")
                of_ = sb3("of_", [128, 512], F32)
                sq_ = sb3("sq_", [128, 512], BF16)
                rstd_ = sb3("rstd_", [128, 512], F32)
                r_of, r_sq, r_rstd = Res("of_"), Res("sq_"), Res("rstd_")

                def load_hws(h):
                    k = h % 2
                    tiles = [12 + h, 16 + h, 20 + h, 24 + h]
                    S.dma([(hws[k][:, i, :, :], w_in_bf[tl]) for i, tl in enumerate(tiles)],
                          reads=[r_win], writes=[r_hws[k]])

                def finalize(h):
                    k = h % 2
                    S.dma([(G[k], cc_out[h].rearrange("(r p) f -> p r f", p=128))],
                          reads=[r_ccout[h]], writes=[r_G[k]])
                    memset("vector", Sin, 0.0, [r_Sin])
                    tt("vector", deff, G[k][:, :, 128], pms, ALU.mult, [r_G[k], r_const], [r_Sin])
                    tt("vector", deff, deff, ompm, ALU.add, [r_Sin, r_const], [r_Sin])
                    for r in range(8):
                        ts("vector", tmpS, G[k][:, r, 0:128], pms[:, r:r + 1], None, ALU.mult, None,
                           [r_G[k], r_const], [r_Sin])
                        stt(Sin, Sin, deff[:, r:r + 1], tmpS, ALU.mult, ALU.add, [r_Sin], [r_Sin])
                    cp("vector", Sinb, Sin, [r_Sin], [r_Sin])
                    for b4 in range(4):
                        bl = slice(512 * b4, 512 * (b4 + 1))
                        pc, r_pc = PS.get()
                        mm(pc, Sinb, qb_p[k][:, bl], True, True, [r_Sin, r_qbp[k]], [r_pc], signal=True)
                        tt("vector", of_, pc, o_p[k][:, bl], ALU.add, [r_pc, r_op[k]], [r_of])
                        act(sq_, of_, AF.Square, [r_of], [r_sq])
                        pn, r_pn = PS.get()
                        mm(pn, ones_bf, sq_, True, True, [r_const, r_sq], [r_pn], signal=True)
                        act(rstd_, pn, AF.Ln, [r_pn], [r_rstd], scale=1.0 / 128, bias=EPS)
                        act(rstd_, rstd_, AF.Exp, [r_rstd], [r_rstd], scale=-0.5)
                        tt("vector", of_, of_, rstd_, ALU.mult, [r_of, r_rstd], [r_of])
                        stt(mixT[:, 4 + h, bl], of_, hnws[:, h:h + 1], sg_p[k][:, bl], ALU.mult, ALU.mult,
                            [r_of, r_const, r_sgp[k]], [r_mix[4 + h]])

                import os as _os
                if _os.environ.get("HG_LIMIT"):
                    S.mark(int(_os.environ["HG_LIMIT"]))
                load_hws(0)
                sbi = 0
                for h in range(4):
                    if h + 1 < 4:
                        load_hws(h + 1)
                    k = h % 2
                    hw_ = hws[k]
                    r_hw = r_hws[k]
                    for tb in range(4):
                        ub = 4 + tb
                        ublk = slice(512 * ub, 512 * (ub + 1))
                        bl = slice(512 * tb, 512 * (tb + 1))
                        pf, r_pf = PS.get()
                        mm_group(pf, [(hw_[:, 1, c, :], uT[:, c, ublk]) for c in range(8)],
                                 [r_hw, r_uT[ub]], [r_pf])
                        act(fS, pf, AF.Sigmoid, [r_pf], [r_fS])
                        pq, r_pq = PS.get()
                        mm_group(pq, [(hw_[:, 0, c, :], uT[:, c, ublk]) for c in range(8)],
                                 [r_hw, r_uT[ub]], [r_pq])
                        act(qf, pq, AF.Silu, [r_pq], [r_qf])
                        pg, r_pg = PS.get()
                        mm_group(pg, [(hw_[:, 3, c, :], uT[:, c, ublk]) for c in range(8)],
                                 [r_hw, r_uT[ub]], [r_pg])
                        act(sg_p[k][:, bl], pg, AF.Silu, [r_pg], [r_sgp[k]])
                        pvv, r_pvv = PS.get()
                        for t4 in range(4):
                            t0 = 512 * ub + 128 * t4
                            for c in range(8):
                                mm(pvv[:, 128 * t4:128 * (t4 + 1)], uT[:, c, t0:t0 + 128], hw_[:, 2, c, :],
                                   c == 0, c == 7, [r_hw, r_uT[ub]], [r_pvv], signal=(c == 7 and t4 == 3))
                        act(Vh[:, :, :].rearrange("p a b -> p (a b)"), pvv, AF.Copy, [r_pvv], [r_Vh])
                        ts("vector", fS, fS, omlb[:, h:h + 1], lbc[:, h:h + 1], ALU.mult, ALU.add,
                           [r_fS, r_const], [r_fS])
                        act(lgf, fS, AF.Ln, [r_fS], [r_lgf])
                        ts("vector", key, fS, -1.0, 1.0, ALU.mult, ALU.add, [r_fS], [r_key])
                        if tb == 0:
                            memset("vector", carry, 0.0, [r_small])
                        S.op("vector", lambda e: e.tensor_tensor_scan(
                            out=Pc, data0=ones_f[:, 0:1].broadcast_to([128, 512]), data1=lgf,
                            initial=carry[:, 0:1], op0=ALU.mult, op1=ALU.add),
                            [r_lgf, r_small, r_const], [r_Pc])
                        cp("vector", Rc[:, 0:1], carry, [r_small], [r_small])
                        cp("vector", Rc[:, 1:8], Pc[:, 63:448:64], [r_Pc, r_small], [r_small])
                        cp("vector", Pe, Pc[:, 63:512:64], [r_Pc, r_small], [r_small])
                        cp("vector", carry, Pc[:, 511:512], [r_Pc, r_small], [r_small])
                        tt("vector", dec, Pe, Rc, ALU.subtract, [r_small], [r_small])
                        act(dec, dec, AF.Exp, [r_small], [r_small])
                        P3 = Pc[:, :].rearrange("p (c t) -> p c t", t=64)
                        tt("vector", bb[:, :].rearrange("p (c t) -> p c t", t=64), P3,
                           Rc[:, :].unsqueeze(2).broadcast_to([128, 8, 64]), ALU.subtract,
                           [r_Pc, r_small], [r_bb])
                        act(ee, bb, AF.Exp, [r_bb], [r_ee])
                        tt("vector", qt_, qf, ee, ALU.mult, [r_qf, r_ee], [r_qt])
                        act(ee, bb, AF.Exp, [r_bb, r_ee], [r_ee], scale=-1.0)
                        tt("gpsimd", kt_, key, ee, ALU.mult, [r_key, r_ee], [r_kt])
                        tt("vector", bb[:, :].rearrange("p (c t) -> p c t", t=64),
                           Pe[:, :].unsqueeze(2).broadcast_to([128, 8, 64]), P3, ALU.subtract,
                           [r_Pc, r_small, r_bb], [r_bb])
                        act(ee, bb, AF.Exp, [r_bb, r_ee], [r_ee])
                        tt("gpsimd", khT, key, ee, ALU.mult, [r_key, r_ee], [r_khT])
                        act(ee, Pc, AF.Exp, [r_Pc, r_ee], [r_ee])
                        tt("vector", qb_p[k][:, bl], qf, ee, ALU.mult, [r_qf, r_ee], [r_qbp[k]])
                        pkt, r_pkt = PS.get()
                        pktb = pkt.bitcast(BF16)
                        for t4 in range(4):
                            tr(pktb[:, 128 * t4:128 * (t4 + 1)], khT[:, 128 * t4:128 * (t4 + 1)], ident,
                               [r_khT, r_const], [r_pkt], signal=(t4 == 3))
                        act(khat[:, :, :].rearrange("p a b -> p (a b)"), pktb[:, 0:512], AF.Copy,
                            [r_pkt], [r_khat])
                        pa, r_pa = PS.get()
                        for t4 in range(4):
                            cs = slice(128 * t4, 128 * (t4 + 1))
                            mm(pa[:, cs], kt_[:, cs], qt_[:, cs], True, True, [r_kt, r_qt], [r_pa],
                               signal=(t4 == 3))
                        tt("vector", ATm, pa, hmask, ALU.mult, [r_pa, r_const], [r_ATm])
                        pkv = [PS.get(), PS.get()]
                        for c8 in range(8):
                            p0 = 64 * (c8 % 2)
                            t4 = c8 // 2
                            bank, r_bank = pkv[c8 % 2]
                            mm(bank[:, 128 * t4:128 * (t4 + 1)], khat[p0:p0 + 64, t4, :],
                               Vh[p0:p0 + 64, t4, :], True, True, [r_khat, r_Vh], [r_bank],
                               signal=(c8 >= 6))
                        po, r_po = PS.get()
                        for c8 in range(8):
                            p0 = 64 * (c8 % 2)
                            t4 = c8 // 2
                            gc = 8 * tb + c8
                            cs = slice(64 * c8, 64 * (c8 + 1))
                            bank, r_bank = pkv[c8 % 2]
                            kvs_ = bank[:, 128 * t4:128 * (t4 + 1)]
                            if gc > 0:
                                sb_prev = Sbf[(sbi - 1) % 2]
                                r_sbp = r_Sbf[(sbi - 1) % 2]
                                mm(po[:, cs], sb_prev, qt_[:, cs], True, False, [r_sbp, r_qt], [r_po],
                                   signal=False)
                                mm(po[:, cs], Vh[p0:p0 + 64, t4, :], ATm[p0:p0 + 64, cs], False, True,
                                   [r_Vh, r_ATm], [r_po], signal=True)
                                stt(S32, S32, dec[:, c8:c8 + 1], kvs_, ALU.mult, ALU.add,
                                    [r_S32, r_small, r_bank], [r_S32])
                            else:
                                mm(po[:, cs], Vh[p0:p0 + 64, t4, :], ATm[p0:p0 + 64, cs], True, True,
                                   [r_Vh, r_ATm], [r_po], signal=True)
                                cp("vector", S32, kvs_, [r_bank], [r_S32])
                            cp("gpsimd", Sbf[sbi % 2], S32, [r_S32], [r_Sbf[sbi % 2]])
                            sbi += 1
                        act(o_p[k][:, bl], po, AF.Copy, [r_po], [r_op[k]])
                    cp("vector", ccs[k][:, 0:128], S32, [r_S32], [r_ccs[k]])
                    act(ccs[k][:, 128:129], carry, AF.Exp, [r_small], [r_ccs[k]])
                    if stop_after == 3:
                        t = dbg_tensor("dbg_op", [128, TOWN], F32)
                        out_stores.append(S.dma([(t, o_p[0])], reads=[r_op[0]]))
                        t = dbg_tensor("dbg_U", [128, 129], F32)
                        out_stores.append(S.dma([(t, ccs[0])], reads=[r_ccs[0]]))
                        return _finish(nc, S, out_stores, dbg_out)
                    S.dma([(cc_in[h], ccs[k])], reads=[r_ccs[k]], writes=[r_ccin[h]])
                    S.async_op("gpsimd", lambda e, h=h: e.collective_compute(
                        "AllGather", ALU.bypass, replica_groups=[list(range(NCORES))],
                        ins=[cc_in[h]], outs=[cc_out[h]]), [r_ccin[h]], [r_ccout[h]])
                    if dbg == "hgrn" and h == 0:
                        t = dbg_tensor("dbg_op", [128, TOWN], F32)
                        out_stores.append(S.dma([(t, o_p[0])], reads=[r_op[0]]))
                        t = dbg_tensor("dbg_U", [128, 129], F32)
                        out_stores.append(S.dma([(t, ccs[0])], reads=[r_ccs[0]]))
                    if h >= 1:
                        finalize(h - 1)
                finalize(3)
                S.barrier()
        if dbg == "mix":
            t = dbg_tensor("dbg_mix", [128, 8, TOWN], BF16)
            out_stores.append(S.dma([(t, mixT)], reads=r_mix))
        if stop_after <= 4:
            return _finish(nc, S, out_stores, dbg_out)

        with contextlib.ExitStack() as ph5:
            def sb5(name, shape, dt):
                return ph5.enter_context(nc.sbuf_tensor(name, list(shape), dt)).ap()
            wo = sb5("wo", [128, 8, 1024], BF16)
            wd = sb5("wd", [128, NFC, 1024], BF16)
            r_wo, r_wd = Res("wo"), Res("wd")
            S.dma([(wo, w_out_bf)], reads=[r_wout], writes=[r_wo])
            S.dma([(wd[:, 0:11, :], w_down_bf[:, 0:11, :]), (wd[:, 11:22, :], w_down_bf[:, 11:22, :])],
                  reads=[r_wdown], writes=[r_wd])
            fwb = sb5("fwb", [128, D], F32)
            S.dma([(fwb, fw.partition_broadcast(128))], writes=[r_const])
            hb_ = sb5("hb_", [128, 4, D], F32)
            r_hb = [Res(f"hb{i}") for i in range(4)]
            u2T = sb5("u2T", [128, 8, 512], BF16)
            r_u2 = Res("u2T")
            actT = sb5("actT", [128, NFC, 512], BF16)
            r_actT = [Res(f"actT{i}") for i in range(NFC)]
            wgu = [sb5(f"wgu{i}", [128, 2, 8, 128], BF16) for i in range(3)]
            r_wgu_s = [Res(f"wgu{i}") for i in range(3)]
            xr = [sb5(f"xr{i}", [128, D], F32) for i in range(2)]
            r_xr = [Res(f"xr{i}") for i in range(2)]
            xn2 = [sb5(f"xn2{i}", [128, D], BF16) for i in range(2)]
            r_xn2 = [Res(f"xn2{i}") for i in range(2)]
            junk5 = sb5("junk5", [128, D], BF16)
            r_junk5 = Res("junk5")
            ot = [sb5(f"ot{i}", [128, D], F32) for i in range(2)]
            r_ot = [Res(f"ot{i}") for i in range(2)]
            sgt = [sb5(f"sgt{i}", [128, 512], BF16) for i in range(2)]
            r_sgt = [Res(f"sgt{i}") for i in range(2)]
            ss5 = sb5("ss5", [128, 32], F32)
            rs5 = sb5("rs5", [128, 32], F32)
            r_ss5 = [Res(f"ss5{i}") for i in range(32)]
            wi = 0
            xi = 0
            for TB in range(4):
                for t4 in range(4):
                    ti = 4 * TB + t4
                    tok = slice(128 * ti, 128 * (ti + 1))
                    k = xi % 2
                    xi += 1
                    S.dma([(xr[k], x_all[TOWN + 128 * ti:TOWN + 128 * (ti + 1), :])], writes=[r_xr[k]])
                    halves = []
                    for hf in range(2):
                        ph, r_ph = PS.get()
                        mm_group(ph, [(mixT[:, c, tok], wo[:, c, 512 * hf:512 * (hf + 1)]) for c in range(8)],
                                 list(r_mix) + [r_wo], [r_ph])
                        halves.append((ph, r_ph))
                    for hf in range(2):
                        ph, r_ph = halves[hf]
                        tt("vector", hb_[:, t4, 512 * hf:512 * (hf + 1)], ph, xr[k][:, 512 * hf:512 * (hf + 1)],
                           ALU.add, [r_ph, r_xr[k]], [r_hb[t4]])
                    si = 2 * ti
                    act(junk5, hb_[:, t4, :], AF.Square, [r_hb[t4]], [r_junk5, r_ss5[ti]],
                        accum_out=ss5[:, ti:ti + 1])
                    act(rs5[:, ti:ti + 1], ss5[:, ti:ti + 1], AF.Ln, [r_ss5[ti]], [r_ss5[ti]],
                        scale=1.0 / D, bias=EPS)
                    act(rs5[:, ti:ti + 1], rs5[:, ti:ti + 1], AF.Exp, [r_ss5[ti]], [r_ss5[ti]], scale=-0.5)
                    kk = ti % 2
                    act(xn2[kk], hb_[:, t4, :], AF.Copy, [r_hb[t4], r_ss5[ti]], [r_xn2[kk]],
                        scale=rs5[:, ti:ti + 1])
                    pb, r_pb = PS.get()
                    pbb = pb.bitcast(BF16)
                    for c in range(8):
                        tr(pbb[:, 128 * c:128 * (c + 1)], xn2[kk][:, 128 * c:128 * (c + 1)], ident,
                           [r_xn2[kk], r_const], [r_pb], signal=(c == 7))
                    tt("vector", u2T[:, :, 128 * t4:128 * (t4 + 1)],
                       pbb.rearrange("p (c t) -> p c t", c=8),
                       n2s[:, :].unsqueeze(2).broadcast_to([128, 8, 128]), ALU.mult,
                       [r_pb, r_const], [r_u2])
                for fc in range(NFC):
                    k = wi % 3
                    wi += 1
                    S.dma([(wgu[k][:, g, :, :], w_gu_bf[fc, g]) for g in range(2)],
                          reads=[r_wgu], writes=[r_wgu_s[k]])
                    pg, r_pg = PS.get()
                    mm_group(pg, [(wgu[k][:, 0, c, :], u2T[:, c, :]) for c in range(8)],
                             [r_wgu_s[k], r_u2], [r_pg])
                    pu, r_pu = PS.get()
                    mm_group(pu, [(wgu[k][:, 1, c, :], u2T[:, c, :]) for c in range(8)],
                             [r_wgu_s[k], r_u2], [r_pu])
                    kk = fc % 2
                    act(sgt[kk], pg, AF.Silu, [r_pg], [r_sgt[kk]])
                    tt("vector", actT[:, fc, :], pu, sgt[kk], ALU.mult, [r_pu, r_sgt[kk]], [r_actT[fc]])
                for t4 in range(4):
                    ti = 4 * TB + t4
                    halves = []
                    for hf in range(2):
                        pd, r_pd = PS.get()
                        mm_group(pd, [(actT[:, fc, 128 * t4:128 * (t4 + 1)], wd[:, fc, 512 * hf:512 * (hf + 1)])
                                      for fc in range(NFC)], list(r_actT) + [r_wd], [r_pd])
                        halves.append((pd, r_pd))
                    for hf in range(2):
                        pd, r_pd = halves[hf]
                        hs = slice(512 * hf, 512 * (hf + 1))
                        tt("vector", hb_[:, t4, hs], pd, hb_[:, t4, hs], ALU.add, [r_pd, r_hb[t4]], [r_hb[t4]])
                    si = 16 + ti
                    act(junk5, hb_[:, t4, :], AF.Square, [r_hb[t4]], [r_junk5, r_ss5[si]],
                        accum_out=ss5[:, si:si + 1])
                    act(rs5[:, si:si + 1], ss5[:, si:si + 1], AF.Ln, [r_ss5[si]], [r_ss5[si]],
                        scale=1.0 / D, bias=EPS)
                    act(rs5[:, si:si + 1], rs5[:, si:si + 1], AF.Exp, [r_ss5[si]], [r_ss5[si]], scale=-0.5)
                    ko = ti % 2
                    stt(ot[ko], hb_[:, t4, :], rs5[:, si:si + 1], fwb, ALU.mult, ALU.mult,
                        [r_hb[t4], r_ss5[si], r_const], [r_ot[ko]])
                    out_stores.append(S.dma([(out[128 * ti:128 * (ti + 1), :], ot[ko])], reads=[r_ot[ko]]))
            S.barrier()
    return _finish(nc, S, out_stores, dbg_out)


def _finish(nc, S, out_stores, dbg_out):
    S.barrier()
    S.run()
    nc._dbg_out = dbg_out
    nc._sched = S
    return nc


def _rope_tables(base):
    half = 32
    inv_freq = (np.float32(10000.0) ** (-np.arange(half, dtype=np.float32) / np.float32(half))).astype(np.float32)
    pos = (np.arange(TALL, dtype=np.float32) + np.float32(base)).astype(np.float32)
    ang = (pos[None, :] * inv_freq[:, None]).astype(np.float32)
    cos = np.cos(ang).astype(np.float32)
    sin = np.sin(ang).astype(np.float32)
    cosT = np.tile(cos, (4, 1))
    sgn = np.where((np.arange(128) % 64) < 32, -1.0, 1.0).astype(np.float32)
    sinT = np.tile(sin, (4, 1)) * sgn[:, None]
    return np.ascontiguousarray(cosT), np.ascontiguousarray(sinT.astype(np.float32))


def _tile_cols(w):
    n = w.shape[1]
    return np.ascontiguousarray(w.reshape(8, 128, n // 128, 128).transpose(2, 1, 0, 3))


def prepare_inputs(x, norm1_w, w_in, lb_logits, hgrn_norm_w, w_out, norm2_w, w_gate_up, w_down,
                   final_norm_w):
    f32 = np.float32
    x = np.asarray(x, f32)
    w_in0 = np.asarray(w_in, f32)[0]
    qk = w_in0[:, :1024]
    qk_sw = qk.reshape(1024, 16, 2, 32)[:, :, ::-1, :].reshape(1024, 1024)
    w_inT = np.concatenate([_tile_cols(w_in0), _tile_cols(qk_sw)], axis=0)
    w_outL = np.ascontiguousarray(np.asarray(w_out, f32)[0].reshape(8, 128, 1024).transpose(1, 0, 2))
    wgu = np.asarray(w_gate_up, f32)[0]
    g_t = _tile_cols(wgu[:, :FF])
    u_t = _tile_cols(wgu[:, FF:])
    w_guT = np.ascontiguousarray(np.stack([g_t, u_t], axis=1))
    w_downL = np.ascontiguousarray(np.asarray(w_down, f32)[0].reshape(NFC, 128, 1024).transpose(1, 0, 2))
    c_ident = np.eye(128, dtype=f32)
    p = np.arange(128)[:, None]
    q = np.arange(128)[None, :]
    cur = (p <= q).astype(f32)
    prev = (p >= q).astype(f32)
    c_amask = np.concatenate([cur, prev, cur, prev], axis=1)
    hm = ((p // 64 == q // 64) & (p <= q)).astype(f32)
    c_hmask = np.concatenate([hm] * 4, axis=1)
    n1 = np.ascontiguousarray(np.asarray(norm1_w, f32)[0].reshape(8, 128).T)
    n2 = np.ascontiguousarray(np.asarray(norm2_w, f32)[0].reshape(8, 128).T)
    hnw = np.ascontiguousarray(np.asarray(hgrn_norm_w, f32)[0].reshape(4, 128).T)
    lbl_ = np.asarray(lb_logits, f32)
    lbl = np.ascontiguousarray(np.concatenate([lbl_[0].reshape(4, 128).T, lbl_[1].reshape(4, 128).T], axis=1))
    fw = np.ascontiguousarray(np.asarray(final_norm_w, f32))
    shared = dict(w_inT=w_inT, w_outL=w_outL, w_guT=w_guT, w_downL=w_downL, c_ident=c_ident,
                  c_amask=c_amask, c_hmask=c_hmask, n1=n1, n2=n2, hnw=hnw, lbl=lbl, fw=fw)
    in_maps = []
    for c in range(NCORES):
        b, j = divmod(c, 4)
        x_all = np.zeros((TALL, D), f32)
        if j > 0:
            x_all[:TOWN] = x[b, TOWN * (j - 1):TOWN * j]
        x_all[TOWN:] = x[b, TOWN * j:TOWN * (j + 1)]
        cosT, sinT = _rope_tables(TOWN * (j - 1))
        pmv = np.zeros((128, 8), f32)
        for r in range(4 * b, 4 * b + j):
            pmv[:, r] = 1.0
        kv = np.ones((128, NVT), f32)
        if j == 0:
            for (d, r, ip), vt in VT_INDEX.items():
                if ip == 0:
                    kv[:, vt] = 0.0
        m = dict(shared)
        tag = np.float32(c)
        m["w_inT"] = np.concatenate([w_inT, np.full((1, 128, 8, 128), tag, f32)], axis=0)
        m["w_outL"] = np.concatenate([w_outL, np.full((128, 1, 1024), tag, f32)], axis=1)
        m["w_guT"] = np.concatenate([w_guT, np.full((1, 2, 128, 8, 128), tag, f32)], axis=0)
        m["w_downL"] = np.concatenate([w_downL, np.full((128, 1, 1024), tag, f32)], axis=1)
        m.update(x_all=x_all, cosT=cosT, sinT=sinT, pm=pmv, kval=kv)
        in_maps.append(m)
    return in_maps


_NC_CACHE = {}


def kernel(x, norm1_w, w_in, lb_logits, hgrn_norm_w, w_out, norm2_w, w_gate_up, w_down, final_norm_w):
    in_maps = prepare_inputs(x, norm1_w, w_in, lb_logits, hgrn_norm_w, w_out, norm2_w, w_gate_up,
                             w_down, final_norm_w)
    nc = build_nc()
    res = run_bass_kernel_spmd(nc, in_maps, core_ids=list(range(NCORES)))
    outp = np.zeros((2, 8192, D), np.float32)
    for c in range(NCORES):
        b, j = divmod(c, 4)
        outp[b, TOWN * j:TOWN * (j + 1)] = np.asarray(res.results[c]["out"], np.float32)
    return outp
```
